# Optimizing a Trainium2 kernel written in Bass

```python
import math
import jax, jax.numpy as jnp
from jax import lax
import numpy as np

D_MODEL = 1024
BATCH = 4
SEQ = 8192
DEPTH = 2

CHUNK = 64
EPS = 1e-6
GDN_HEADS = 4
GDN_DK = 128
GDN_DV = 128
GDN_CONV = 4
GDN_QK = GDN_HEADS * GDN_DK
GDN_W = GDN_HEADS * GDN_DV
MLSTM_HEADS = 4
MLSTM_DH = 64
MLSTM_CONV = 4
MLSTM_W = MLSTM_HEADS * MLSTM_DH
S5_GROUPS = 16
S5_GROUP_CH = 16
S5_STATE = 64
S5_W = S5_GROUPS * S5_GROUP_CH
D_MIX = GDN_W + MLSTM_W + S5_W
D_FF = 2816
FFN_CONV = 3
IN_SPLITS = (GDN_QK, GDN_QK, GDN_W, GDN_W, GDN_HEADS, GDN_HEADS,
             MLSTM_W, MLSTM_W, MLSTM_W, MLSTM_W, MLSTM_HEADS, MLSTM_HEADS,
             S5_W)
D_IN = sum(IN_SPLITS)

kernel_name = 'hybrid_gdn_mlstm_s5_parallel_heads'


def rms_norm(x, w):
    x32 = x.astype(jnp.float32)
    y = x32 * lax.rsqrt(jnp.mean(x32 * x32, axis=-1, keepdims=True) + EPS) * w.astype(jnp.float32)
    return y.astype(x.dtype)


def head_rms_norm(x, w):
    return x * lax.rsqrt(jnp.mean(x * x, axis=-1, keepdims=True) + EPS) * w.astype(jnp.float32)


def l2_norm(x):
    return x * lax.rsqrt(jnp.sum(x * x, axis=-1, keepdims=True) + EPS)


def causal_depthwise_conv(x, w):
    k_w, ch = w.shape
    return lax.conv_general_dilated(
        x, w[:, None, :].astype(x.dtype), window_strides=(1,), padding=((k_w - 1, 0),),
        dimension_numbers=('NWC', 'WIO', 'NWC'), feature_group_count=ch)


def to_chunks(x):
    b_, l_, h_, d_ = x.shape
    return x.reshape(b_, l_ // CHUNK, CHUNK, h_, d_).transpose(1, 0, 3, 2, 4)


def gates_to_chunks(g):
    b_, l_, h_ = g.shape
    return g.reshape(b_, l_ // CHUNK, CHUNK, h_).transpose(1, 0, 3, 2)


def from_chunks(x):
    n_, b_, h_, c_, d_ = x.shape
    return x.transpose(1, 0, 3, 2, 4).reshape(b_, n_ * c_, h_, d_)


def gated_delta_rule_chunked(q, k, v, g, beta):
    b_, l_, h_, dk = q.shape
    dv = v.shape[-1]
    qc, kc, vc = to_chunks(q), to_chunks(k), to_chunks(v)
    gc = jnp.cumsum(gates_to_chunks(g), axis=-1)
    bc = gates_to_chunks(beta)
    tril = jnp.tril(jnp.ones((CHUNK, CHUNK), bool))
    strict = jnp.tril(jnp.ones((CHUNK, CHUNK), bool), k=-1)
    decay = jnp.exp(jnp.where(tril, gc[..., :, None] - gc[..., None, :], -jnp.inf))
    kb = kc * bc[..., None]
    m = jnp.where(strict, jnp.einsum('nbhtd,nbhsd->nbhts', kb, kc) * decay, 0.0)
    eye = jnp.eye(CHUNK, dtype=jnp.float32)
    t_inv = lax.linalg.triangular_solve(eye + m, jnp.broadcast_to(eye, m.shape),
                                        left_side=True, lower=True)
    u = jnp.einsum('nbhts,nbhsv->nbhtv', t_inv, vc * bc[..., None])
    w = jnp.einsum('nbhts,nbhsk->nbhtk', t_inv, kb * jnp.exp(gc)[..., None])
    a_qk = jnp.einsum('nbhtd,nbhsd->nbhts', qc, kc) * decay
    q_dec = qc * jnp.exp(gc)[..., None]
    k_dec = kc * jnp.exp(gc[..., -1:] - gc)[..., None]
    g_last = jnp.exp(gc[..., -1])

    def step(s, xs):
        q_i, k_i, u_i, w_i, a_i, gl = xs
        v_new = u_i - jnp.einsum('bhtk,bhkv->bhtv', w_i, s)
        o = jnp.einsum('bhtk,bhkv->bhtv', q_i, s) + jnp.einsum('bhts,bhsv->bhtv', a_i, v_new)
        s = s * gl[..., None, None] + jnp.einsum('bhsk,bhsv->bhkv', k_i, v_new)
        return s, o

    s0 = jnp.zeros((b_, h_, dk, dv), jnp.float32)
    _, o = lax.scan(step, s0, (q_dec, k_dec, u, w, a_qk, g_last))
    return from_chunks(o)


def gated_deltanet(q, k, v, z, b_pre, a_pre, conv_w, a_log, dt_bias, norm_w):
    b_, l_, _ = q.shape
    qkv = jax.nn.silu(causal_depthwise_conv(jnp.concatenate([q, k, v], axis=-1), conv_w))
    q, k, v = jnp.split(qkv.astype(jnp.float32), [GDN_QK, 2 * GDN_QK], axis=-1)
    q = l2_norm(q.reshape(b_, l_, GDN_HEADS, GDN_DK)) * (GDN_DK ** -0.5)
    k = l2_norm(k.reshape(b_, l_, GDN_HEADS, GDN_DK))
    v = v.reshape(b_, l_, GDN_HEADS, GDN_DV)
    beta = jax.nn.sigmoid(b_pre.astype(jnp.float32))
    g = -jnp.exp(a_log.astype(jnp.float32)) * jax.nn.softplus(
        a_pre.astype(jnp.float32) + dt_bias.astype(jnp.float32))
    o = gated_delta_rule_chunked(q, k, v, g, beta)
    o = head_rms_norm(o, norm_w) * jax.nn.silu(
        z.astype(jnp.float32).reshape(b_, l_, GDN_HEADS, GDN_DV))
    return o.reshape(b_, l_, GDN_W)


def mlstm_chunked(q, k, v, i_gate, log_f):
    b_, l_, h_, d_ = q.shape
    qc, kc, vc = to_chunks(q), to_chunks(k), to_chunks(v)
    ic = gates_to_chunks(i_gate)
    bcum = jnp.cumsum(gates_to_chunks(log_f), axis=-1)
    tril = jnp.tril(jnp.ones((CHUNK, CHUNK), bool))
    log_w = jnp.where(tril, bcum[..., :, None] - bcum[..., None, :] + ic[..., None, :], -jnp.inf)
    m_intra = jnp.max(log_w, axis=-1)
    qk = jnp.einsum('nbhtd,nbhsd->nbhts', qc, kc)
    log_end = bcum[..., -1:] - bcum + ic
    m_end = jnp.max(log_end, axis=-1)

    def step(carry, xs):
        c_st, n_st, m_st = carry
        q_i, k_i, v_i, b_i, lw_i, mi_i, qk_i, le_i, me_i = xs
        log_inter = b_i + m_st[..., None]
        m_t = jnp.maximum(log_inter, mi_i)
        w_inter = jnp.exp(log_inter - m_t)
        w_intra = jnp.exp(lw_i - m_t[..., None]) * qk_i
        num = (w_inter[..., None] * jnp.einsum('bhtk,bhkv->bhtv', q_i, c_st)
               + jnp.einsum('bhts,bhsv->bhtv', w_intra, v_i))
        den = w_inter * jnp.einsum('bhtk,bhk->bht', q_i, n_st) + jnp.sum(w_intra, axis=-1)
        h = num / jnp.maximum(jnp.abs(den), jnp.exp(-m_t))[..., None]
        b_last = b_i[..., -1]
        m_new = jnp.maximum(b_last + m_st, me_i)
        a = jnp.exp(b_last + m_st - m_new)
        wk = jnp.exp(le_i - m_new[..., None])[..., None] * k_i
        c_st = a[..., None, None] * c_st + jnp.einsum('bhsk,bhsv->bhkv', wk, v_i)
        n_st = a[..., None] * n_st + jnp.sum(wk, axis=-2)
        return (c_st, n_st, m_new), h

    init = (jnp.zeros((b_, h_, d_, d_), jnp.float32), jnp.zeros((b_, h_, d_), jnp.float32),
            jnp.zeros((b_, h_), jnp.float32))
    _, h = lax.scan(step, init, (qc, kc, vc, bcum, log_w, m_intra, qk, log_end, m_end))
    return from_chunks(h)


def mlstm(q, k, v, o_pre, i_pre, f_pre, conv_w, i_bias, f_bias, norm_w):
    b_, l_, _ = q.shape
    qk = jax.nn.silu(causal_depthwise_conv(jnp.concatenate([q, k], axis=-1), conv_w))
    q, k = jnp.split(qk.astype(jnp.float32), 2, axis=-1)
    q = q.reshape(b_, l_, MLSTM_HEADS, MLSTM_DH) * (MLSTM_DH ** -0.5)
    k = k.reshape(b_, l_, MLSTM_HEADS, MLSTM_DH)
    v = v.astype(jnp.float32).reshape(b_, l_, MLSTM_HEADS, MLSTM_DH)
    i_gate = i_pre.astype(jnp.float32) + i_bias.astype(jnp.float32)
    log_f = jax.nn.log_sigmoid(f_pre.astype(jnp.float32) + f_bias.astype(jnp.float32))
    h = mlstm_chunked(q, k, v, i_gate, log_f)
    o_gate = jax.nn.sigmoid(o_pre.astype(jnp.float32).reshape(b_, l_, MLSTM_HEADS, MLSTM_DH))
    return (o_gate * head_rms_norm(h, norm_w)).reshape(b_, l_, MLSTM_W)


def s5_mixer(u, lam_re, lam_im, log_step, b_re, b_im, c_re, c_im, d_skip, w_glu):
    b_, l_, _ = u.shape
    u32 = u.astype(jnp.float32).reshape(b_, l_, S5_GROUPS, S5_GROUP_CH)
    lr = lam_re.astype(jnp.float32)
    li = lam_im.astype(jnp.float32)
    step = jnp.exp(log_step.astype(jnp.float32))
    er = jnp.exp(lr * step)
    abar_re = er * jnp.cos(li * step)
    abar_im = er * jnp.sin(li * step)
    den = lr * lr + li * li
    coef_re = ((abar_re - 1.0) * lr + abar_im * li) / den
    coef_im = (abar_im * lr - (abar_re - 1.0) * li) / den
    br = b_re.astype(jnp.float32)
    bi = b_im.astype(jnp.float32)
    bb_re = coef_re[..., None] * br - coef_im[..., None] * bi
    bb_im = coef_re[..., None] * bi + coef_im[..., None] * br
    bu_re = jnp.einsum('blgh,gph->blgp', u32, bb_re)
    bu_im = jnp.einsum('blgh,gph->blgp', u32, bb_im)
    a_re = jnp.broadcast_to(abar_re, bu_re.shape)
    a_im = jnp.broadcast_to(abar_im, bu_im.shape)

    def combine(e1, e2):
        a1r, a1i, b1r, b1i = e1
        a2r, a2i, b2r, b2i = e2
        return (a2r * a1r - a2i * a1i, a2r * a1i + a2i * a1r,
                a2r * b1r - a2i * b1i + b2r, a2r * b1i + a2i * b1r + b2i)

    _, _, x_re, x_im = lax.associative_scan(combine, (a_re, a_im, bu_re, bu_im), axis=1)
    y = (jnp.einsum('gjp,blgp->blgj', c_re.astype(jnp.float32), x_re)
         - jnp.einsum('gjp,blgp->blgj', c_im.astype(jnp.float32), x_im))
    y = y.reshape(b_, l_, S5_W) + d_skip.astype(jnp.float32) * u32.reshape(b_, l_, S5_W)
    g = jax.nn.gelu(y, approximate=False)
    return g * jax.nn.sigmoid(g @ w_glu.astype(jnp.float32))


def conv_ffn(x, w_up, conv_w, conv_b, w_down):
    h = causal_depthwise_conv(x @ w_up, conv_w) + conv_b
    gate, up = jnp.split(h, 2, axis=-1)
    return (jax.nn.silu(gate) * up) @ w_down


def setup_inputs(seed: int = 0) -> dict:
    key = jax.random.key(seed)
    ks = jax.random.split(key, 32)
    f32 = jnp.float32

    def nrm(k, shape, scale):
        return jax.random.normal(k, shape, f32) * scale

    nl = DEPTH
    x = nrm(ks[0], (BATCH, SEQ, D_MODEL), 1.0)
    norm1_w = 1.0 + nrm(ks[1], (nl, D_MODEL), 0.02)
    w_in = nrm(ks[2], (nl, D_MODEL, D_IN), D_MODEL ** -0.5)
    gdn_conv_w = nrm(ks[3], (nl, GDN_CONV, 2 * GDN_QK + GDN_W), GDN_CONV ** -0.5)
    gdn_a_log = jnp.log(jax.random.uniform(ks[4], (nl, GDN_HEADS), f32, 1.0, 16.0))
    dt = jnp.exp(jax.random.uniform(ks[5], (nl, GDN_HEADS), f32, math.log(1e-3), math.log(1e-1)))
    gdn_dt_bias = dt + jnp.log(-jnp.expm1(-dt))
    gdn_norm_w = 1.0 + nrm(ks[6], (nl, GDN_DV), 0.02)
    mlstm_conv_w = nrm(ks[7], (nl, MLSTM_CONV, 2 * MLSTM_W), MLSTM_CONV ** -0.5)
    mlstm_i_bias = nrm(ks[8], (nl, MLSTM_HEADS), 0.1)
    mlstm_f_bias = jnp.linspace(3.0, 6.0, MLSTM_HEADS, dtype=f32)[None, :] + nrm(ks[9], (nl, MLSTM_HEADS), 0.1)
    mlstm_norm_w = 1.0 + nrm(ks[10], (nl, MLSTM_DH), 0.02)
    s5_lam_re = -0.5 + nrm(ks[11], (nl, S5_GROUPS, S5_STATE), 0.01)
    s5_lam_im = (math.pi * jnp.arange(S5_STATE, dtype=f32))[None, None, :] + nrm(ks[12], (nl, S5_GROUPS, S5_STATE), 0.01)
    s5_log_step = jax.random.uniform(ks[13], (nl, S5_GROUPS, S5_STATE), f32, math.log(1e-3), math.log(1e-1))
    s5_b_re = nrm(ks[14], (nl, S5_GROUPS, S5_STATE, S5_GROUP_CH), (2 * S5_GROUP_CH) ** -0.5)
    s5_b_im = nrm(ks[15], (nl, S5_GROUPS, S5_STATE, S5_GROUP_CH), (2 * S5_GROUP_CH) ** -0.5)
    s5_c_re = nrm(ks[16], (nl, S5_GROUPS, S5_GROUP_CH, S5_STATE), (2 * S5_STATE) ** -0.5)
    s5_c_im = nrm(ks[17], (nl, S5_GROUPS, S5_GROUP_CH, S5_STATE), (2 * S5_STATE) ** -0.5)
    s5_d = nrm(ks[18], (nl, S5_W), 1.0)
    s5_w_glu = nrm(ks[19], (nl, S5_W, S5_W), S5_W ** -0.5)
    w_out = nrm(ks[20], (nl, D_MIX, D_MODEL), D_MIX ** -0.5)
    norm2_w = 1.0 + nrm(ks[21], (nl, D_MODEL), 0.02)
    w_up = nrm(ks[22], (nl, D_MODEL, 2 * D_FF), D_MODEL ** -0.5)
    ffn_conv_w = nrm(ks[23], (nl, FFN_CONV, 2 * D_FF), FFN_CONV ** -0.5)
    ffn_conv_b = nrm(ks[24], (nl, 2 * D_FF), 0.01)
    w_down = nrm(ks[25], (nl, D_FF, D_MODEL), D_FF ** -0.5)
    final_norm_w = 1.0 + nrm(ks[26], (D_MODEL,), 0.02)
    return {'x': x, 'norm1_w': norm1_w, 'w_in': w_in, 'gdn_conv_w': gdn_conv_w,
            'gdn_a_log': gdn_a_log, 'gdn_dt_bias': gdn_dt_bias, 'gdn_norm_w': gdn_norm_w,
            'mlstm_conv_w': mlstm_conv_w, 'mlstm_i_bias': mlstm_i_bias, 'mlstm_f_bias': mlstm_f_bias,
            'mlstm_norm_w': mlstm_norm_w, 's5_lam_re': s5_lam_re, 's5_lam_im': s5_lam_im,
            's5_log_step': s5_log_step, 's5_b_re': s5_b_re, 's5_b_im': s5_b_im,
            's5_c_re': s5_c_re, 's5_c_im': s5_c_im, 's5_d': s5_d, 's5_w_glu': s5_w_glu,
            'w_out': w_out, 'norm2_w': norm2_w, 'w_up': w_up, 'ffn_conv_w': ffn_conv_w,
            'ffn_conv_b': ffn_conv_b, 'w_down': w_down, 'final_norm_w': final_norm_w}


def reference(x, norm1_w, w_in, gdn_conv_w, gdn_a_log, gdn_dt_bias, gdn_norm_w,
              mlstm_conv_w, mlstm_i_bias, mlstm_f_bias, mlstm_norm_w,
              s5_lam_re, s5_lam_im, s5_log_step, s5_b_re, s5_b_im, s5_c_re, s5_c_im,
              s5_d, s5_w_glu, w_out, norm2_w, w_up, ffn_conv_w, ffn_conv_b, w_down,
              final_norm_w):
    split_idx = [int(s) for s in np.cumsum(IN_SPLITS)[:-1]]
    for l in range(DEPTH):
        h = rms_norm(x, norm1_w[l])
        proj = h @ w_in[l]
        (g_q, g_k, g_v, g_z, g_b, g_a,
         m_q, m_k, m_v, m_o, m_i, m_f, s_u) = jnp.split(proj, split_idx, axis=-1)
        y_a = gated_deltanet(g_q, g_k, g_v, g_z, g_b, g_a, gdn_conv_w[l], gdn_a_log[l],
                             gdn_dt_bias[l], gdn_norm_w[l])
        y_b = mlstm(m_q, m_k, m_v, m_o, m_i, m_f, mlstm_conv_w[l], mlstm_i_bias[l],
                    mlstm_f_bias[l], mlstm_norm_w[l])
        y_c = s5_mixer(s_u, s5_lam_re[l], s5_lam_im[l], s5_log_step[l], s5_b_re[l], s5_b_im[l],
                       s5_c_re[l], s5_c_im[l], s5_d[l], s5_w_glu[l])
        y = jnp.concatenate([y_a, y_b, y_c], axis=-1).astype(x.dtype)
        x = x + y @ w_out[l]
        x = x + conv_ffn(rms_norm(x, norm2_w[l]), w_up[l], ffn_conv_w[l], ffn_conv_b[l], w_down[l])
    return rms_norm(x, final_norm_w)
```

```python
import math
from contextlib import ExitStack
import numpy as np
import concourse.bass as bass
import concourse.mybir as mybir
from concourse.bass_utils import run_bass_kernel_spmd

F32 = mybir.dt.float32
BF16 = mybir.dt.bfloat16
I32 = mybir.dt.int32
AF = mybir.ActivationFunctionType
ALU = mybir.AluOpType
EPS = 1e-6
NEG = -1.0e30
TWO_PI = 2.0 * math.pi


class Sched:
    ENGS = ("pe", "dve", "act", "pool", "sp")

    def __init__(self, nc):
        self.nc = nc
        self.q = {e: [] for e in self.ENGS}
        self.cnt = {}
        self.last_w = {}
        self.reads = {}
        self.seen = {e: {} for e in self.ENGS}
        self.semkeys = set(self.ENGS)

    def _deps(self, eng, reads, writes):
        deps = {}

        def add(ev):
            if ev is None:
                return
            k, v = ev
            if deps.get(k, 0) < v:
                deps[k] = v
        for r in reads:
            add(self.last_w.get(r))
        for w in writes:
            add(self.last_w.get(w))
            for ev in self.reads.get(w, ()):
                add(ev)
        out = []
        for k, v in deps.items():
            if k == eng and eng == "pe":
                continue
            if self.seen[eng].get(k, 0) >= v:
                continue
            self.seen[eng][k] = v
            out.append((k, v))
        return out

    def _record(self, ev, reads, writes):
        for r in reads:
            self.reads.setdefault(r, []).append(ev)
        for w in writes:
            self.last_w[w] = ev
            self.reads[w] = []

    def op(self, eng, fn, reads=(), writes=()):
        waits = self._deps(eng, reads, writes)
        self.cnt[eng] = self.cnt.get(eng, 0) + 1
        ev = (eng, self.cnt[eng])
        self.q[eng].append((fn, waits, (eng, 1)))
        self._record(ev, reads, writes)

    def dma(self, eng, slot, fn, reads=(), writes=()):
        self.semkeys.add(slot)
        waits = self._deps(eng, reads, writes)
        self.cnt[slot] = self.cnt.get(slot, 0) + 16
        ev = (slot, self.cnt[slot])
        self.q[eng].append((fn, waits, (slot, 16)))
        self._record(ev, reads, writes)
        return ev

    def wait_event(self, eng, ev):
        self.q[eng].append((None, [ev], None))

    def finish(self, es):
        nc = self.nc
        for k in sorted(self.cnt):
            if k.startswith("st"):
                self.wait_event("sp", (k, self.cnt[k]))
        sems = {k: es.enter_context(nc.semaphore(k)) for k in sorted(self.semkeys)}
        block = es.enter_context(nc.Block())

        def mk(engname):
            def body(e):
                for fn, waits, inc in self.q[engname]:
                    for k, v in waits:
                        e.wait_ge(sems[k], v)
                    if fn is not None:
                        fn(e).then_inc(sems[inc[0]], inc[1])
            return body
        block.tensor(mk("pe"))
        block.vector(mk("dve"))
        block.scalar(mk("act"))
        block.gpsimd(mk("pool"))
        block.sync(mk("sp"))


class Ctx:
    def __init__(self, nc, es):
        self.nc = nc
        self.es = es
        self.S = Sched(nc)
        self.nbank = 0

    def sb(self, name, shape, dt=F32):
        return self.es.enter_context(self.nc.sbuf_tensor(name, shape, dt))

    def bank(self, name, shape, dt=F32):
        return self.es.enter_context(self.nc.psum_tensor(name, shape, dt))

    def mm(self, out, lhsT, rhs, r, w, start=True, stop=True):
        self.S.op("pe", lambda e: e.matmul(out, lhsT=lhsT, rhs=rhs, start=start, stop=stop), reads=r, writes=w)

    def tr(self, out, in_, ident, r, w):
        self.S.op("pe", lambda e: e.transpose(out, in_, ident), reads=r, writes=w)

    def act(self, out, in_, func, r, w, bias=0.0, scale=1.0, accum=None):
        self.S.op("act", lambda e: e.activation(out=out, in_=in_, func=func, bias=bias, scale=scale, accum_out=accum), reads=r, writes=w)

    def ts(self, eng, out, in0, s1, s2, op0, op1, r, w):
        if s2 is None:
            self.S.op(eng, lambda e: e.tensor_scalar(out=out, in0=in0, scalar1=s1, scalar2=None, op0=op0), reads=r, writes=w)
        else:
            self.S.op(eng, lambda e: e.tensor_scalar(out=out, in0=in0, scalar1=s1, scalar2=s2, op0=op0, op1=op1), reads=r, writes=w)

    def tt(self, eng, out, in0, in1, op, r, w):
        self.S.op(eng, lambda e: e.tensor_tensor(out=out, in0=in0, in1=in1, op=op), reads=r, writes=w)

    def stt(self, eng, out, in0, s, in1, op0, op1, r, w):
        self.S.op(eng, lambda e: e.scalar_tensor_tensor(out=out, in0=in0, scalar=s, in1=in1, op0=op0, op1=op1), reads=r, writes=w)

    def cp(self, eng, out, in_, r, w):
        if eng == "act":
            self.S.op("act", lambda e: e.copy(out=out, in_=in_), reads=r, writes=w)
        else:
            self.S.op(eng, lambda e: e.tensor_copy(out=out, in_=in_), reads=r, writes=w)

    def recip(self, out, in_, r, w):
        self.S.op("dve", lambda e: e.reciprocal(out=out, in_=in_), reads=r, writes=w)

    def memset(self, eng, ap, val, w):
        self.S.op(eng, lambda e: e.memset(ap, val), writes=w)

    def load(self, slot, out, in_, w, r=()):
        if slot == "ldc":
            self.nbank += 1
            slot = "ldc%d" % self.nbank
        self.S.dma("sp", slot, lambda e: e.dma_start(out=out, in_=in_), reads=r, writes=w)

    def store(self, slot, out, in_, r, w=()):
        self.S.dma("sp", slot, lambda e: e.dma_start(out=out, in_=in_), reads=r, writes=w)

    def consts(self):
        nc = self.nc
        self.ident = self.sb("ident", [128, 128])
        self.identb = self.sb("identb", [128, 128], BF16)
        self.ones = self.sb("ones", [128, 128])
        self.ut = self.sb("ut", [128, 128])
        self.nm_strict = self.sb("nm_strict", [128, 128])
        self.nmT_incl = self.sb("nmT_incl", [128, 128])
        S = self.S
        self.memset("pool", self.ones[:], 1.0, ["ones"])
        self.memset("pool", self.ident[:], 1.0, ["ident"])
        S.op("pool", lambda e: e.affine_select(out=self.ident[:], in_=self.ident[:], pattern=[[-1, 128]], compare_op=ALU.is_equal, fill=0.0, base=0, channel_multiplier=1), reads=["ident"], writes=["ident"])
        self.cp("dve", self.identb[:], self.ident[:], ["ident"], ["identb"])
        self.memset("pool", self.ut[:], 1.0, ["ut"])
        S.op("pool", lambda e: e.affine_select(out=self.ut[:], in_=self.ut[:], pattern=[[1, 128]], compare_op=ALU.is_ge, fill=0.0, base=0, channel_multiplier=-1), reads=["ut"], writes=["ut"])
        self.memset("pool", self.nm_strict[:], 0.0, ["nm_strict"])
        S.op("pool", lambda e: e.affine_select(out=self.nm_strict[:], in_=self.nm_strict[:], pattern=[[-1, 128]], compare_op=ALU.is_gt, fill=NEG, base=0, channel_multiplier=1), reads=["nm_strict"], writes=["nm_strict"])
        self.memset("pool", self.nmT_incl[:], 0.0, ["nmT_incl"])
        S.op("pool", lambda e: e.affine_select(out=self.nmT_incl[:], in_=self.nmT_incl[:], pattern=[[1, 128]], compare_op=ALU.is_ge, fill=NEG, base=0, channel_multiplier=-1), reads=["nmT_incl"], writes=["nmT_incl"])


T = 512
NCH = 4
T5 = 256


def build_A(L, parts=("gdn", "mlstm", "s5")):
    nc = bass.Bass("TRN2", target_bir_lowering=False)
    dr = lambda n, s, k="ExternalInput": nc.dram_tensor(n, s, F32, kind=k).ap()
    x = dr("x", [L, 1024])
    n1w = dr("n1w", [1, 1024])
    wf = dr("wf", [1024, 1152])
    wt = dr("wt", [1024, 520])
    cwd = dr("cw", [128, 32])
    gbias = dr("gbias", [1, 32])
    alog = dr("alog", [1, 8])
    gnw = dr("gnw", [1, 256])
    mnw = dr("mnw", [1, 128])
    s5p = dr("s5p", [128, 12])
    s5b = dr("s5b", [128, 8 * 128])
    s5c = dr("s5c", [128, 8 * 128])
    s5d = dr("s5d", [128, 1])
    yT = dr("yT", [512, L], "ExternalOutput")
    ntile = L // T
    with ExitStack() as es:
        C = Ctx(nc, es)
        S = C.S
        C.consts()
        sb = C.sb
        Btr = C.bank("Btr", [128, 8, 128], BF16)
        Bf = C.bank("Bf", [128, 512])
        Bt = C.bank("Bt", [128, 512])
        Bg = C.bank("Bg", [128, 512])
        Bd = C.bank("Bd", [128, 512])
        Bp = C.bank("Bp", [128, 512])
        Br = C.bank("Br", [128, 512])
        Bs = C.bank("Bs", [128, 512])
        wfb = sb("wfb", [128, 8, 1152], BF16)
        wtb = sb("wtb", [128, 8, 520], BF16)
        stage = sb("stage", [128, 1152])
        for kc in range(8):
            C.load("ldw", stage[:, 0:1152], wf[kc * 128:(kc + 1) * 128, :], ["stage"])
            C.cp("pool", wfb[:, kc, :], stage[:, 0:1152], ["stage"], ["wfb"])
            C.load("ldw", stage[:, 0:520], wt[kc * 128:(kc + 1) * 128, :], ["stage"])
            C.cp("pool", wtb[:, kc, :], stage[:, 0:520], ["stage"], ["wtb"])
        n1b = sb("n1b", [128, 1024])
        C.load("ldc", n1b[:], n1w.partition_broadcast(128), ["n1b"])
        cw = sb("cw_sb", [128, 8, 4])
        C.load("ldc", cw[:].rearrange("p a b -> p (a b)"), cwd[:, :], ["cw"])
        gb = sb("gb", [128, 8, 4])
        C.load("ldc", gb[:].rearrange("p a b -> p (a b)"), gbias.partition_broadcast(128), ["gb"])
        nA = sb("nA", [128, 2, 4])
        C.load("ldc", nA[:].rearrange("p a b -> p (a b)"), alog.partition_broadcast(128), ["nA"])
        C.act(nA[:], nA[:], AF.Exp, ["nA"], ["nA"])
        C.ts("dve", nA[:], nA[:], -1.0, None, ALU.mult, None, ["nA"], ["nA"])
        gnb = sb("gnb", [128, 256])
        C.load("ldc", gnb[:], gnw.partition_broadcast(128), ["gnb"])
        mnb = sb("mnb", [128, 128])
        C.load("ldc", mnb[:], mnw.partition_broadcast(128), ["mnb"])
        pc = sb("pc", [128, 8, 3 + T])
        C.memset("pool", pc[:, :, 0:3], 0.0, ["pc"])
        Sg = sb("Sg", [128, 2, 128])
        C.memset("pool", Sg[:], 0.0, ["Sg"])
        Cm = sb("Cm", [128, 130])
        C.memset("pool", Cm[:], 0.0, ["Cm"])
        vaug = sb("vaug", [128, NCH, 2, 65])
        C.memset("pool", vaug[:], 1.0, ["vaug"])
        if "s5" in parts:
            s5 = s5_setup(C, s5p, s5b, s5c, s5d)
            C._s5y = Bs
        xin = [sb("xin0", [128, 1024]), sb("xin1", [128, 1024])]
        hb = sb("hb", [128, 1024], BF16)
        sq_junk = sb("sq_junk", [128, 1024])
        ssq = sb("ssq", [128, 2])
        hT = sb("hT", [128, 8, T], BF16)
        cv = sb("cv", [128, 8, T])
        su = sb("su", [128, T])
        tmpF = sb("tmpF", [128, T])
        tmpF2 = sb("tmpF2", [128, T])
        zg = sb("zg", [128, NCH, 256])
        og = sb("og", [128, NCH, 128])
        gt = sb("gt", [128, 8, NCH])
        gx = sb("gx", [128, 8, NCH])
        csin = sb("csin", [128, 4, NCH])
        cs = sb("cs", [128, 2, 4, NCH])
        gcol = sb("gcol", [128, 12, 2, NCH])
        yout = sb("yout", [128, 4, T])
        ytok = sb("ytok", [128, 3, 128])

        for it in range(ntile):
            t0 = it * T
            for c in range(NCH):
                xi = xin[c % 2]
                xn = "xin%d" % (c % 2)
                C.load("ldx%d" % (c % 2), xi[:], x[t0 + c * 128: t0 + (c + 1) * 128, :], [xn])
                C.act(sq_junk[:], xi[:], AF.Square, [xn], ["sq_junk", "ssq"], accum=ssq[:, 0:1])
                C.act(ssq[:, 1:2], ssq[:, 0:1], AF.Sqrt, ["ssq"], ["ssq"], bias=EPS, scale=1.0 / 1024)
                C.recip(ssq[:, 1:2], ssq[:, 1:2], ["ssq"], ["ssq"])
                C.stt("dve", hb[:], xi[:], ssq[:, 1:2], n1b[:], ALU.mult, ALU.mult, [xn, "ssq", "n1b"], ["hb"])
                for kc in range(8):
                    C.tr(Btr[:, kc, :], hb[:, kc * 128:(kc + 1) * 128], C.identb[:], ["hb", "identb"], ["Btr"])
                C.cp("act", hT[:, :, c * 128:(c + 1) * 128], Btr[:], [], ["Btr", "hT"])
            C.cp("pool", pc[:, :, 0:3], pc[:, :, T:T + 3], ["pc"], ["pc"]) if it > 0 else None
            for blk in range(9):
                for kc in range(8):
                    C.mm(Bf[:], wfb[:, kc, blk * 128:(blk + 1) * 128], hT[:, kc, :], ["wfb", "hT"], ["Bf"], start=(kc == 0), stop=(kc == 7))
                if blk < 8:
                    C.cp("act", pc[:, blk, 3:3 + T], Bf[:], [], ["Bf", "pc"])
                else:
                    C.cp("act", su[:], Bf[:], [], ["Bf", "su"])
            for c in range(NCH):
                for kc in range(8):
                    C.mm(Bt[:], hT[:, kc, c * 128:(c + 1) * 128], wtb[:, kc, 0:512], ["wtb", "hT"], ["Bt"], start=(kc == 0), stop=(kc == 7))
                for kc in range(8):
                    C.mm(Bg[:, c * 8:(c + 1) * 8], hT[:, kc, c * 128:(c + 1) * 128], wtb[:, kc, 512:520], ["wtb", "hT"], ["Bg"], start=(kc == 0), stop=(kc == 7))
                C.act(zg[:, c, :], Bt[:, 0:256], AF.Silu, [], ["Bt", "zg"])
                C.tt("pool", zg[:, c, :], zg[:, c, :], gnb[:], ALU.mult, ["gnb"], ["zg"])
                C.cp("dve", vaug[:, c, :, 0:64], Bt[:, 256:384].rearrange("p (h d) -> p h d", h=2), [], ["Bt", "vaug"])
                C.act(og[:, c, :], Bt[:, 384:512], AF.Sigmoid, [], ["Bt", "og"])
                C.tt("pool", og[:, c, :], og[:, c, :], mnb[:], ALU.mult, ["mnb"], ["og"])
            C.tt("dve", gt[:], Bg[:, 0:32].rearrange("p (c g) -> p g c", g=8), gb[:], ALU.add, ["gb"], ["Bg", "gt"])
            beta = gcol[:, 0]
            lnb = gcol[:, 1]
            C.act(beta, gt[:, 0:2, :], AF.Sigmoid, ["gt"], ["gcol"])
            C.act(lnb, beta, AF.Ln, ["gcol"], ["gcol"])
            C.act(gx[:, 2:4, :], gt[:, 2:4, :], AF.Exp, ["gt"], ["gx"])
            C.act(gx[:, 2:4, :], gx[:, 2:4, :], AF.Ln, ["gx"], ["gx"], bias=1.0)
            C.tt("dve", csin[:, 0:2, :], gx[:, 2:4, :], nA[:], ALU.mult, ["gx", "nA"], ["csin"])
            C.act(gx[:, 6:8, :], gt[:, 6:8, :], AF.Exp, ["gt"], ["gx"], scale=-1.0)
            C.act(gx[:, 6:8, :], gx[:, 6:8, :], AF.Ln, ["gx"], ["gx"], bias=1.0)
            C.ts("dve", csin[:, 2:4, :], gx[:, 6:8, :], -1.0, None, ALU.mult, None, ["gx"], ["csin"])
            csin_f = csin[:].rearrange("p a b -> p (a b)")
            C.mm(Bg[:, 64:80], C.ut[:], csin_f, ["ut", "csin"], ["Bg"])
            C.mm(Bg[:, 80:96], C.ones[:], csin_f, ["ones", "csin"], ["Bg"])
            C.cp("dve", cs[:].rearrange("p a b c -> p (a b c)"), Bg[:, 64:96], [], ["Bg", "cs"])
            gc = cs[:, 0, 0:2, :]
            gtot = cs[:, 1, 0:2, :]
            bcm = cs[:, 0, 2:4, :]
            btot = cs[:, 1, 2:4, :]
            col2, ngc, kdec, bgc, gl = gcol[:, 2], gcol[:, 3], gcol[:, 4], gcol[:, 5], gcol[:, 6]
            colb, wkc, am, nbc = gcol[:, 7], gcol[:, 8], gcol[:, 9], gcol[:, 10]
            C.tt("dve", col2, gc, lnb, ALU.add, ["cs", "gcol"], ["gcol"])
            C.ts("dve", ngc, gc, -1.0, None, ALU.mult, None, ["cs"], ["gcol"])
            C.tt("dve", kdec, gtot, gc, ALU.subtract, ["cs"], ["gcol"])
            C.act(kdec, kdec, AF.Exp, ["gcol"], ["gcol"])
            C.act(bgc, col2, AF.Exp, ["gcol"], ["gcol"])
            C.act(gl, gtot, AF.Exp, ["cs"], ["gcol"])
            C.tt("dve", colb, gt[:, 4:6, :], bcm, ALU.subtract, ["gt", "cs"], ["gcol"])
            C.tt("dve", wkc, colb, btot, ALU.add, ["gcol", "cs"], ["gcol"])
            C.act(wkc, wkc, AF.Exp, ["gcol"], ["gcol"])
            C.act(am, btot, AF.Exp, ["cs"], ["gcol"])
            for blk in range(8):
                C.ts("dve", cv[:, blk, :], pc[:, blk, 3:3 + T], cw[:, blk, 3:4], None, ALU.mult, None, ["pc", "cw"], ["cv"])
                for j in range(3):
                    C.stt("dve", cv[:, blk, :], pc[:, blk, j:j + T], cw[:, blk, j:j + 1], cv[:, blk, :], ALU.mult, ALU.add, ["pc", "cw", "cv"], ["cv"])
            C.act(cv[:].rearrange("p a b -> p (a b)"), cv[:].rearrange("p a b -> p (a b)"), AF.Silu, ["cv"], ["cv"])
            if "gdn" in parts:
                for blk in range(4):
                    C.act(tmpF[:], cv[:, blk, :], AF.Square, ["cv"], ["tmpF"])
                    C.mm(Bf[:], C.ones[:], tmpF[:], ["ones", "tmpF"], ["Bf"])
                    if blk < 2:
                        C.act(tmpF2[:], Bf[:], AF.Sqrt, [], ["Bf", "tmpF2"], bias=128.0 * EPS, scale=128.0)
                    else:
                        C.act(tmpF2[:], Bf[:], AF.Sqrt, [], ["Bf", "tmpF2"], bias=EPS, scale=1.0)
                    C.recip(tmpF2[:], tmpF2[:], ["tmpF2"], ["tmpF2"])
                    C.tt("dve", cv[:, blk, :], cv[:, blk, :], tmpF2[:], ALU.mult, ["cv", "tmpF2"], ["cv"])
            for c in range(NCH):
                cs_ = slice(c * 128, (c + 1) * 128)
                if "gdn" in parts:
                    gdn_chunk(C, c, cs_, cv, gcol, cs, zg, Sg, ytok, Bg, Bd, Bp, Br, Bs)
                else:
                    C.memset("pool", ytok[:, 0:2, :], 0.0, ["ytok"])
                if "mlstm" in parts:
                    mlstm_chunk(C, c, cs_, cv, gcol, cs, og, vaug, Cm, ytok, Bg, Bd, Br, Bs)
                else:
                    C.memset("pool", ytok[:, 2, :], 0.0, ["ytok"])
                for b3 in range(3):
                    C.tr(Br[:, b3 * 128:(b3 + 1) * 128], ytok[:, b3, :], C.ident[:], ["ytok", "ident"], ["Br"])
                C.cp("act", yout[:, 0:3, cs_], Br[:, 0:384].rearrange("p (a b) -> p a b", a=3), [], ["Br", "yout"])
            if "s5" in parts:
                s5_tile(C, s5, su, yout, Bf, Bt)
            else:
                C.memset("pool", yout[:, 3, :], 0.0, ["yout"])
            for b4 in range(4):
                C.store("sty%d" % b4, yT[b4 * 128:(b4 + 1) * 128, t0:t0 + T], yout[:, b4, :], ["yout"])
        S.finish(es)
    return nc


def gdn_chunk(C, c, cs_, cv, gcol, cs, zg, Sg, ytok, Bg, Bd, Bp, Br, Bs):
    C._gc = cs[:, 0, 0:2, :]
    sb = C.sb
    if not hasattr(C, "_gdn"):
        C._gdn = dict(
            E=sb("gE", [128, 4, 128]), Pa=[sb("gPa0", [128, 2, 128]), sb("gPa1", [128, 2, 128])],
            Pb=[sb("gPb0", [128, 2, 128]), sb("gPb1", [128, 2, 128])],
            R=[sb("gR0", [128, 2, 128]), sb("gR1", [128, 2, 128])],
            AT=sb("gAT", [128, 2, 128]), kbg=sb("gkbg", [128, 2, 128]), kdc=sb("gkdc", [128, 2, 128]),
            bv=sb("gbv", [128, 2, 128]), wT=sb("gwT", [128, 2, 128]), u=sb("gu", [128, 2, 128]),
            grow=sb("ggrow", [128, 2, 128]), qd=sb("gqd", [128, 2, 128]), vn=sb("gvn", [128, 2, 128]),
            ss=sb("gss", [128, 4]), junk=sb("gjunk", [128, 128]))
    G = C._gdn
    E, AT = G["E"], G["AT"]
    beta, col2, ngc, kdec, bgc, gl = (gcol[:, i] for i in (0, 2, 3, 4, 5, 6))
    col = lambda a, h: a[:, h, c:c + 1]
    I = C.ident
    for h in range(2):
        kT = cv[:, 2 + h, cs_]
        qT = cv[:, 0 + h, cs_]
        C.mm(Bp[:, h * 128:(h + 1) * 128], kT, kT, ["cv"], ["Bp"])
        C.mm(Bp[:, (2 + h) * 128:(3 + h) * 128], kT, qT, ["cv"], ["Bp"])
    for h in range(2):
        C.mm(Bd[:, h * 128:(h + 1) * 128], col(ngc, h).to_broadcast([128, 128]), I[:], ["gcol", "ident"], ["Bd"], start=True, stop=False)
        C.mm(Bd[:, h * 128:(h + 1) * 128], I[:], C.nm_strict[:], ["ident", "nm_strict"], ["Bd"], start=False, stop=True)
    for h in range(2):
        C.mm(Bd[:, (2 + h) * 128:(3 + h) * 128], col(C._gc, h).to_broadcast([128, 128]), I[:], ["cs", "ident"], ["Bd"], start=True, stop=False)
        C.mm(Bd[:, (2 + h) * 128:(3 + h) * 128], I[:], C.nmT_incl[:], ["ident", "nmT_incl"], ["Bd"], start=False, stop=True)
    for h in range(2):
        C.act(E[:, h, :], Bd[:, h * 128:(h + 1) * 128], AF.Exp, ["gcol"], ["Bd", "gE"], bias=col(col2, h))
        C.act(E[:, 2 + h, :], Bd[:, (2 + h) * 128:(3 + h) * 128], AF.Exp, ["gcol"], ["Bd", "gE"], bias=col(ngc, h))
    for h in range(2):
        C.mm(Bd[:, h * 128:(h + 1) * 128], col(C._gc, h).to_broadcast([128, 128]), I[:], ["cs", "ident"], ["Bd"])
    C.act(G["grow"][:].rearrange("p a b -> p (a b)"), Bd[:, 0:256], AF.Exp, [], ["Bd", "ggrow"])
    C.tt("dve", G["qd"][:], cv[:, 0:2, cs_], G["grow"][:], ALU.mult, ["cv", "ggrow"], ["gqd"])
    Pa, Pb, R = G["Pa"], G["Pb"], G["R"]
    C.stt("dve", Pa[0][:].rearrange("p a b -> p (a b)"), Bp[:, 0:256], -1.0, E[:, 0:2, :].rearrange("p a b -> p (a b)"), ALU.mult, ALU.mult, ["gE"], ["Bp", "gPa0"])
    C.tt("dve", AT[:].rearrange("p a b -> p (a b)"), Bp[:, 256:512], E[:, 2:4, :].rearrange("p a b -> p (a b)"), ALU.mult, ["gE"], ["Bp", "gAT"])
    for h in range(2):
        C.tr(Br[:, h * 128:(h + 1) * 128], Pa[0][:, h, :], I[:], ["gPa0", "ident"], ["Br"])
    C.cp("act", Pb[0][:].rearrange("p a b -> p (a b)"), Br[:, 0:256], [], ["Br", "gPb0"])
    for h in range(2):
        C.tt("dve", R[0][:, h, :], Pb[0][:, h, :], I[:], ALU.add, ["gPb0", "ident"], ["gR0"])
    cur = 0
    for j in range(1, 7):
        nxt = 1 - cur
        pa, pb, pan, pbn = "gPa%d" % cur, "gPb%d" % cur, "gPa%d" % nxt, "gPb%d" % nxt
        for h in range(2):
            C.mm(Bp[:, h * 128:(h + 1) * 128], Pb[cur][:, h, :], Pa[cur][:, h, :], [pa, pb], ["Bp"])
            if j < 6:
                C.mm(Bp[:, (2 + h) * 128:(3 + h) * 128], Pa[cur][:, h, :], Pb[cur][:, h, :], [pa, pb], ["Bp"])
        C.cp("act", Pa[nxt][:].rearrange("p a b -> p (a b)"), Bp[:, 0:256], [], ["Bp", pan])
        if j < 6:
            C.cp("dve", Pb[nxt][:].rearrange("p a b -> p (a b)"), Bp[:, 256:512], [], ["Bp", pbn])
        for h in range(2):
            C.mm(Br[:, h * 128:(h + 1) * 128], Pa[nxt][:, h, :], R[cur][:, h, :], [pan, "gR%d" % cur], ["Br"])
        C.tt("dve", R[nxt][:].rearrange("p a b -> p (a b)"), Br[:, 0:256], R[cur][:].rearrange("p a b -> p (a b)"), ALU.add, ["gR%d" % cur], ["Br", "gR%d" % nxt])
        cur = nxt
    Tt = R[cur]
    Tn = "gR%d" % cur
    for h in range(2):
        C.tr(Bg[:, 128 + h * 128:256 + h * 128], cv[:, 2 + h, cs_], I[:], ["cv", "ident"], ["Bg"])
    for h in range(2):
        C.ts("dve", G["kbg"][:, h, :], Bg[:, 128 + h * 128:256 + h * 128], col(bgc, h), None, ALU.mult, None, ["gcol"], ["Bg", "gkbg"])
        C.act(G["kdc"][:, h, :], Bg[:, 128 + h * 128:256 + h * 128], AF.Copy, ["gcol"], ["Bg", "gkdc"], scale=col(kdec, h))
    for h in range(2):
        C.tr(Bg[:, 128 + h * 128:256 + h * 128], cv[:, 4 + h, cs_], I[:], ["cv", "ident"], ["Bg"])
    for h in range(2):
        C.ts("dve", G["bv"][:, h, :], Bg[:, 128 + h * 128:256 + h * 128], col(beta, h), None, ALU.mult, None, ["gcol"], ["Bg", "gbv"])
    for h in range(2):
        C.mm(Bd[:, h * 128:(h + 1) * 128], G["kbg"][:, h, :], Tt[:, h, :], ["gkbg", Tn], ["Bd"])
        C.mm(Bd[:, (2 + h) * 128:(3 + h) * 128], Tt[:, h, :], G["bv"][:, h, :], ["gbv", Tn], ["Bd"])
    C.cp("act", G["wT"][:].rearrange("p a b -> p (a b)"), Bd[:, 0:256], [], ["Bd", "gwT"])
    C.cp("dve", G["u"][:].rearrange("p a b -> p (a b)"), Bd[:, 256:512], [], ["Bd", "gu"])
    for h in range(2):
        C.mm(Bs[:, h * 128:(h + 1) * 128], G["wT"][:, h, :], Sg[:, h, :], ["gwT", "Sg"], ["Bs"])
    C.tt("dve", G["vn"][:].rearrange("p a b -> p (a b)"), G["u"][:].rearrange("p a b -> p (a b)"), Bs[:, 0:256], ALU.subtract, ["gu"], ["Bs", "gvn"])
    for h in range(2):
        C.mm(Bs[:, h * 128:(h + 1) * 128], G["qd"][:, h, :], Sg[:, h, :], ["gqd", "Sg"], ["Bs"], start=True, stop=False)
        C.mm(Bs[:, h * 128:(h + 1) * 128], AT[:, h, :], G["vn"][:, h, :], ["gAT", "gvn"], ["Bs"], start=False, stop=True)
    for h in range(2):
        C.mm(Bs[:, (2 + h) * 128:(3 + h) * 128], G["kdc"][:, h, :], G["vn"][:, h, :], ["gkdc", "gvn"], ["Bs"])
    for h in range(2):
        C.stt("dve", Sg[:, h, :], Sg[:, h, :], col(gl, h), Bs[:, (2 + h) * 128:(3 + h) * 128], ALU.mult, ALU.add, ["Sg", "gcol"], ["Bs", "Sg"])
    ss = G["ss"]
    for h in range(2):
        C.act(G["junk"][:], Bs[:, h * 128:(h + 1) * 128], AF.Square, [], ["Bs", "gjunk", "gss"], accum=ss[:, h:h + 1])
    C.act(ss[:, 2:4], ss[:, 0:2], AF.Sqrt, ["gss"], ["gss"], bias=EPS, scale=1.0 / 128)
    C.recip(ss[:, 2:4], ss[:, 2:4], ["gss"], ["gss"])
    for h in range(2):
        C.stt("dve", ytok[:, h, :], Bs[:, h * 128:(h + 1) * 128], ss[:, 2 + h:3 + h], zg[:, c, h * 128:(h + 1) * 128], ALU.mult, ALU.mult, ["gss", "zg"], ["Bs", "ytok"])


def mlstm_chunk(C, c, cs_, cv, gcol, cs, og, vaug, Cm, ytok, Bg, Bd, Br, Bs):
    sb = C.sb
    if not hasattr(C, "_ml"):
        C._ml = dict(ET=sb("mET", [128, 2, 128]), WT=sb("mWT", [128, 2, 128]), wk=sb("mwk", [128, 128]),
                     grow=sb("mgrow", [128, 128]), qd=sb("mqd", [128, 128]), hh=sb("mhh", [128, 2, 64]),
                     ss=sb("mss", [128, 8]), junk=sb("mjunk", [128, 64]))
    M = C._ml
    I = C.ident
    colb, wkc, am = gcol[:, 7], gcol[:, 8], gcol[:, 9]
    bcm = cs[:, 0, 2:4, :]
    col = lambda a, h: a[:, h, c:c + 1]
    qT = cv[:, 6, cs_]
    kT = cv[:, 7, cs_]
    for h in range(2):
        hs = slice(h * 64, (h + 1) * 64)
        C.mm(Bd[:, h * 128:(h + 1) * 128], kT[hs, :], qT[hs, :], ["cv"], ["Bd"])
        C.mm(Bd[:, (2 + h) * 128:(3 + h) * 128], col(bcm, h).to_broadcast([128, 128]), I[:], ["cs", "ident"], ["Bd"], start=True, stop=False)
        C.mm(Bd[:, (2 + h) * 128:(3 + h) * 128], I[:], C.nmT_incl[:], ["ident", "nmT_incl"], ["Bd"], start=False, stop=True)
    for h in range(2):
        C.act(M["ET"][:, h, :], Bd[:, (2 + h) * 128:(3 + h) * 128], AF.Exp, ["gcol"], ["Bd", "mET"], bias=col(colb, h))
    C.stt("dve", M["WT"][:].rearrange("p a b -> p (a b)"), Bd[:, 0:256], 0.125, M["ET"][:].rearrange("p a b -> p (a b)"), ALU.mult, ALU.mult, ["mET"], ["Bd", "mWT"])
    for h in range(2):
        C.mm(Bd[:, h * 128:(h + 1) * 128], col(bcm, h).to_broadcast([128, 128]), I[:], ["cs", "ident"], ["Bd"])
    for h in range(2):
        hs = slice(h * 64, (h + 1) * 64)
        C.act(M["grow"][hs, :], Bd[hs, h * 128:(h + 1) * 128], AF.Exp, [], ["Bd", "mgrow"])
    C.stt("dve", M["qd"][:], qT, 0.125, M["grow"][:], ALU.mult, ALU.mult, ["cv", "mgrow"], ["mqd"])
    C.tr(Br[:, 384:512], kT, I[:], ["cv", "ident"], ["Br"])
    for h in range(2):
        C.ts("dve", M["wk"][:, h * 64:(h + 1) * 64], Br[:, 384 + h * 64:384 + (h + 1) * 64], col(wkc, h), None, ALU.mult, None, ["gcol"], ["Br", "mwk"])
    va = vaug[:, c].rearrange("p h d -> p (h d)")
    C.mm(Bs[:, 0:130], M["qd"][:], Cm[:], ["mqd", "Cm"], ["Bs"], start=True, stop=False)
    for h in range(2):
        C.mm(Bs[:, h * 65:(h + 1) * 65], M["WT"][:, h, :], va[:, h * 65:(h + 1) * 65], ["mWT", "vaug"], ["Bs"], start=False, stop=(h == 1))
    C.mm(Bs[:, 256:386], M["wk"][:], va, ["mwk", "vaug"], ["Bs"])
    for h in range(2):
        hs = slice(h * 64, (h + 1) * 64)
        C.stt("dve", Cm[hs, h * 65:(h + 1) * 65], Cm[hs, h * 65:(h + 1) * 65], am[hs, h, c:c + 1], Bs[hs, 256 + h * 65:256 + (h + 1) * 65], ALU.mult, ALU.add, ["Cm", "gcol"], ["Bs", "Cm"])
    ss = M["ss"]
    num = Bs[:, 0:130].rearrange("p (h d) -> p h d", h=2)
    C.act(ss[:, 0:2], num[:, :, 64], AF.Abs, [], ["Bs", "mss"])
    C.ts("dve", ss[:, 0:2], ss[:, 0:2], 1.0, None, ALU.max, None, ["mss"], ["mss"])
    C.recip(ss[:, 0:2], ss[:, 0:2], ["mss"], ["mss"])
    for h in range(2):
        C.act(M["hh"][:, h, :], num[:, h, 0:64], AF.Copy, ["mss"], ["Bs", "mhh"], scale=ss[:, h:h + 1])
    for h in range(2):
        C.act(M["junk"][:], M["hh"][:, h, :], AF.Square, ["mhh"], ["mjunk", "mss"], accum=ss[:, 2 + h:3 + h])
    C.act(ss[:, 4:6], ss[:, 2:4], AF.Sqrt, ["mss"], ["mss"], bias=EPS, scale=1.0 / 64)
    C.recip(ss[:, 4:6], ss[:, 4:6], ["mss"], ["mss"])
    for h in range(2):
        C.stt("dve", ytok[:, 2, h * 64:(h + 1) * 64], M["hh"][:, h, :], ss[:, 4 + h:5 + h], og[:, c, h * 64:(h + 1) * 64], ALU.mult, ALU.mult, ["mhh", "mss", "og"], ["ytok"])


def s5_setup(C, s5p, s5b, s5c, s5d):
    sb = C.sb
    P = dict()
    prm = sb("s5prm", [128, 3, 4])
    C.load("ldc", prm[:].rearrange("p a b -> p (a b)"), s5p[:, :], ["s5prm"])
    bT = sb("s5bT", [128, 8, 128])
    cT = sb("s5cT", [128, 8, 128])
    ncT = sb("s5ncT", [128, 8, 128])
    dcol = sb("s5dcol", [128, 1])
    C.load("ldc", bT[:].rearrange("p a b -> p (a b)"), s5b[:, :], ["s5bT"])
    C.load("ldc", cT[:].rearrange("p a b -> p (a b)"), s5c[:, :], ["s5cT"])
    C.load("ldc", dcol[:], s5d[:, :], ["s5dcol"])
    C.ts("dve", ncT[:], cT[:], -1.0, None, ALU.mult, None, ["s5cT"], ["s5ncT"])
    w = sb("s5w", [128, 16, 4])
    lr, li, ls = prm[:, 0, :], prm[:, 1, :], prm[:, 2, :]
    step, th, er, sn, cn, cre, cim, den = (w[:, i, :] for i in range(8))
    t1, t2, t3, t4 = (w[:, i, :] for i in range(8, 12))
    rr, ncim = w[:, 12, :], w[:, 13, :]
    rd, wr = ["s5w", "s5prm"], ["s5w"]
    C.act(step, ls, AF.Exp, rd, wr)
    C.tt("dve", th, li, step, ALU.mult, rd, wr)
    C.tt("dve", t1, lr, step, ALU.mult, rd, wr)
    C.act(er, t1, AF.Exp, rd, wr)

    def sincos(dst_s, dst_c, ang, shape, key):
        r = sb("s5r_" + key, shape)
        ri = sb("s5ri_" + key, shape, I32)
        rf = sb("s5rf_" + key, shape)
        for dst, off in ((dst_s, 0.0), (dst_c, 0.25)):
            C.ts("dve", r[:], ang, 1.0 / TWO_PI, off, ALU.mult, ALU.add, rd + ["s5tab"], ["s5r_" + key])
            C.cp("dve", ri[:], r[:], ["s5r_" + key], ["s5ri_" + key])
            C.cp("dve", rf[:], ri[:], ["s5ri_" + key], ["s5rf_" + key])
            C.tt("dve", r[:], r[:], rf[:], ALU.subtract, ["s5rf_" + key], ["s5r_" + key])
            C.act(dst, r[:], AF.Sin, ["s5r_" + key], wr + ["s5tab"], scale=TWO_PI - 2e-6)
    sincos(sn, cn, th, [128, 4], "a")
    C.tt("dve", t1, er, cn, ALU.mult, rd, wr)
    C.tt("dve", t2, er, sn, ALU.mult, rd, wr)
    C.ts("dve", t1, t1, -1.0, None, ALU.add, None, rd, wr)
    C.tt("dve", den, lr, lr, ALU.mult, rd, wr)
    C.tt("dve", t3, li, li, ALU.mult, rd, wr)
    C.tt("dve", den, den, t3, ALU.add, rd, wr)
    C.recip(den, den, rd, wr)
    C.tt("dve", t3, t1, lr, ALU.mult, rd, wr)
    C.tt("dve", t4, t2, li, ALU.mult, rd, wr)
    C.tt("dve", cre, t3, t4, ALU.add, rd, wr)
    C.tt("dve", cre, cre, den, ALU.mult, rd, wr)
    C.tt("dve", t3, t2, lr, ALU.mult, rd, wr)
    C.tt("dve", t4, t1, li, ALU.mult, rd, wr)
    C.tt("dve", cim, t3, t4, ALU.subtract, rd, wr)
    C.tt("dve", cim, cim, den, ALU.mult, rd, wr)
    tau = sb("s5tau", [128, T5])
    S = C.S
    S.op("pool", lambda e: e.iota(tau[:], pattern=[[1, T5]], base=1, channel_multiplier=0, allow_small_or_imprecise_dtypes=True), writes=["s5tau"])
    ang = sb("s5ang", [128, 4, T5])
    cosT = sb("s5cos", [128, 4, T5])
    sinT = sb("s5sin", [128, 4, T5])
    for k in range(4):
        C.ts("dve", ang[:, k, :], tau[:], th[:, k:k + 1], None, ALU.mult, None, ["s5tau"] + rd, ["s5ang"])
    sincos(sinT[:], cosT[:], ang[:], [128, 4, T5], "b")
    Fr = sb("s5Fr", [128, 4, T5])
    Fi = sb("s5Fi", [128, 4, T5])
    tb = ["s5tab"] + rd
    for k in range(4):
        C.ts("dve", Fr[:, k, :], cosT[:, k, :], cre[:, k:k + 1], None, ALU.mult, None, tb, ["s5F"])
        C.stt("dve", Fr[:, k, :], sinT[:, k, :], cim[:, k:k + 1], Fr[:, k, :], ALU.mult, ALU.add, tb + ["s5F"], ["s5F"])
        C.ts("dve", Fi[:, k, :], cosT[:, k, :], cim[:, k:k + 1], None, ALU.mult, None, tb, ["s5F"])
        C.ts("dve", ncim[:, k:k + 1], cre[:, k:k + 1], -1.0, None, ALU.mult, None, rd, wr)
        C.stt("dve", Fi[:, k, :], sinT[:, k, :], ncim[:, k:k + 1], Fi[:, k, :], ALU.mult, ALU.add, tb + ["s5F"] + rd, ["s5F"])
    xst = sb("s5x", [128, 2, 4])
    C.memset("pool", xst[:], 0.0, ["s5x"])
    P.update(bT=bT, cT=cT, ncT=ncT, dcol=dcol, er=er, cosT=cosT, sinT=sinT, Fr=Fr, Fi=Fi, xst=xst,
             wre=sb("s5wre", [128, T5]), wim=sb("s5wim", [128, T5]), t1=sb("s5t1", [128, T5]), t2=sb("s5t2", [128, T5]),
             zre=sb("s5zre", [128, T5]), zim=sb("s5zim", [128, T5]),
             p1=sb("s5p1", [128, T5]), p2=sb("s5p2", [128, T5]), p3=sb("s5p3", [128, T5]), p4=sb("s5p4", [128, T5]),
             yv=sb("s5yv", [128, T]))
    return P


def s5_tile(C, P, su, yout, Bf, Bt):
    S = C.S
    F = ["s5F", "s5tab", "s5w"]
    for k in range(4 * (T // T5)):
      hh = k // 4
      k = k % 4
      hsl = slice(hh * T5, (hh + 1) * T5)
      if True:
        C.mm(Bf[:, 0:T5], P["bT"][:, k, :], su[:, hsl], ["s5bT", "su"], ["Bf"])
        C.mm(Bt[:, 0:T5], P["bT"][:, 4 + k, :], su[:, hsl], ["s5bT", "su"], ["Bt"])
        C.tt("dve", P["wre"][:], Bf[:, 0:T5], P["Fr"][:, k, :], ALU.mult, F, ["Bf", "s5wre"])
        C.tt("dve", P["t1"][:], Bt[:, 0:T5], P["Fi"][:, k, :], ALU.mult, F, ["Bt", "s5t1"])
        C.tt("pool", P["wre"][:], P["wre"][:], P["t1"][:], ALU.subtract, ["s5t1"], ["s5wre"])
        C.tt("dve", P["wim"][:], Bt[:, 0:T5], P["Fr"][:, k, :], ALU.mult, F, ["Bt", "s5wim"])
        C.tt("dve", P["t2"][:], Bf[:, 0:T5], P["Fi"][:, k, :], ALU.mult, F, ["Bf", "s5t2"])
        C.tt("pool", P["wim"][:], P["wim"][:], P["t2"][:], ALU.add, ["s5t2"], ["s5wim"])
        erb = P["er"][:, k:k + 1].to_broadcast([128, T5])
        S.op("dve", lambda e, erb=erb, k=k: e.tensor_tensor_scan(out=P["zre"][:], data0=erb, data1=P["wre"][:], initial=P["xst"][:, 0, k:k + 1], op0=ALU.mult, op1=ALU.add), reads=["s5w", "s5wre", "s5x"], writes=["s5zre"])
        S.op("dve", lambda e, erb=erb, k=k: e.tensor_tensor_scan(out=P["zim"][:], data0=erb, data1=P["wim"][:], initial=P["xst"][:, 1, k:k + 1], op0=ALU.mult, op1=ALU.add), reads=["s5w", "s5wim", "s5x"], writes=["s5zim"])
        C.tt("dve", P["p1"][:], P["zre"][:], P["cosT"][:, k, :], ALU.mult, ["s5zre", "s5tab"], ["s5p1"])
        C.tt("pool", P["p2"][:], P["zim"][:], P["sinT"][:, k, :], ALU.mult, ["s5zim", "s5tab"], ["s5p2"])
        C.tt("dve", P["p3"][:], P["zre"][:], P["sinT"][:, k, :], ALU.mult, ["s5zre", "s5tab"], ["s5p3"])
        C.tt("pool", P["p4"][:], P["zim"][:], P["cosT"][:, k, :], ALU.mult, ["s5zim", "s5tab"], ["s5p4"])
        C.tt("dve", P["xst"][:, 0, k:k + 1], P["p1"][:, T5 - 1:T5], P["p2"][:, T5 - 1:T5], ALU.subtract, ["s5p1", "s5p2"], ["s5x"])
        C.tt("dve", P["xst"][:, 1, k:k + 1], P["p3"][:, T5 - 1:T5], P["p4"][:, T5 - 1:T5], ALU.add, ["s5p3", "s5p4"], ["s5x"])
        yb = C._s5y
        C.mm(yb[:, hsl], P["cT"][:, k, :], P["p1"][:], ["s5cT", "s5p1"], ["Bs"], start=(k == 0), stop=False)
        C.mm(yb[:, hsl], P["ncT"][:, k, :], P["p2"][:], ["s5ncT", "s5p2"], ["Bs"], start=False, stop=False)
        C.mm(yb[:, hsl], P["ncT"][:, 4 + k, :], P["p3"][:], ["s5ncT", "s5p3"], ["Bs"], start=False, stop=False)
        C.mm(yb[:, hsl], P["ncT"][:, 4 + k, :], P["p4"][:], ["s5ncT", "s5p4"], ["Bs"], start=False, stop=(k == 3))
    C.stt("dve", P["yv"][:], su[:], P["dcol"][:, 0:1], C._s5y[:], ALU.mult, ALU.add, ["su", "s5dcol"], ["Bs", "s5yv"])
    C.act(yout[:, 3, :], P["yv"][:], AF.Gelu, ["s5yv"], ["yout"])


def prep_A(inp, l, hf, xb):
    f = np.float32
    w_in = inp["w_in"][l]
    gh = [2 * hf, 2 * hf + 1]
    cols = []
    for base in (0, 512, 1024):
        for h in gh:
            cols.append(np.arange(base + h * 128, base + (h + 1) * 128))
    cols.append(np.arange(2056 + 128 * hf, 2056 + 128 * hf + 128))
    cols.append(np.arange(2312 + 128 * hf, 2312 + 128 * hf + 128))
    cols.append(np.arange(3088 + 128 * hf, 3088 + 128 * hf + 128))
    wf = np.ascontiguousarray(w_in[:, np.concatenate(cols)])
    tcols = [np.arange(1536 + 256 * hf, 1536 + 256 * hf + 256), np.arange(2568 + 128 * hf, 2568 + 128 * hf + 128),
             np.arange(2824 + 128 * hf, 2824 + 128 * hf + 128),
             np.array([2048 + gh[0], 2048 + gh[1], 2052 + gh[0], 2052 + gh[1], 3080 + gh[0], 3080 + gh[1], 3084 + gh[0], 3084 + gh[1]])]
    wt = np.ascontiguousarray(w_in[:, np.concatenate(tcols)])
    gcw = inp["gdn_conv_w"][l]
    mcw = inp["mlstm_conv_w"][l]
    cw = np.zeros((128, 8, 4), f)
    blk = 0
    for base in (0, 512, 1024):
        for h in gh:
            cw[:, blk, :] = gcw[:, base + h * 128: base + (h + 1) * 128].T
            blk += 1
    cw[:, 6, :] = mcw[:, 128 * hf:128 * hf + 128].T
    cw[:, 7, :] = mcw[:, 256 + 128 * hf:256 + 128 * hf + 128].T
    z = np.zeros(2, f)
    b8 = np.concatenate([z, inp["gdn_dt_bias"][l][gh], inp["mlstm_i_bias"][l][gh], inp["mlstm_f_bias"][l][gh]]).astype(f)
    gbias = np.repeat(b8, 4)[None, :]
    alog = np.repeat(inp["gdn_a_log"][l][gh], 4)[None, :].astype(f)
    gnw = np.tile(inp["gdn_norm_w"][l], 2)[None, :].astype(f)
    mnw = np.tile(inp["mlstm_norm_w"][l], 2)[None, :].astype(f)
    s5p = np.zeros((128, 3, 4), f)
    s5b = np.zeros((128, 8, 128), f)
    s5c = np.zeros((128, 8, 128), f)
    for gi in range(8):
        g = 8 * hf + gi
        k = gi // 2
        ps = slice((gi % 2) * 64, (gi % 2) * 64 + 64)
        s5p[ps, 0, k] = inp["s5_lam_re"][l, g]
        s5p[ps, 1, k] = inp["s5_lam_im"][l, g]
        s5p[ps, 2, k] = inp["s5_log_step"][l, g]
        s5b[gi * 16:(gi + 1) * 16, k, ps] = inp["s5_b_re"][l, g].T
        s5b[gi * 16:(gi + 1) * 16, 4 + k, ps] = inp["s5_b_im"][l, g].T
        s5c[ps, k, gi * 16:(gi + 1) * 16] = inp["s5_c_re"][l, g].T
        s5c[ps, 4 + k, gi * 16:(gi + 1) * 16] = inp["s5_c_im"][l, g].T
    s5d = np.ascontiguousarray(inp["s5_d"][l, 128 * hf:128 * hf + 128][:, None]).astype(f)
    return {"x": np.ascontiguousarray(xb, f), "n1w": inp["norm1_w"][l][None, :].astype(f), "wf": wf, "wt": wt,
            "cw": cw.reshape(128, 32), "gbias": gbias, "alog": alog, "gnw": gnw, "mnw": mnw,
            "s5p": s5p.reshape(128, 12), "s5b": s5b.reshape(128, 1024), "s5c": s5c.reshape(128, 1024), "s5d": s5d}


TB = 256


def build_B(NT, final):
    nc = bass.Bass("TRN2", target_bir_lowering=False)
    dr = lambda n, s, k="ExternalInput": nc.dram_tensor(n, s, F32, kind=k).ap()
    ntok = NT * TB
    x = dr("x", [ntok, 1024])
    yT = dr("yT", [1024, ntok])
    wout = dr("wout", [1024, 1024])
    wglu = dr("wglu", [256, 256])
    n2w = dr("n2w", [1, 1024])
    fnw = dr("fnw", [1, 1024])
    wup = dr("wup", [1024, 5632])
    fcw = dr("fcw", [128, 44 * 3])
    fcb = dr("fcb", [128, 44])
    wdown = dr("wdown", [2816, 1024])
    xo = dr("xo", [(NT - 1) * TB, 1024], "ExternalOutput")
    with ExitStack() as es:
        C = Ctx(nc, es)
        S = C.S
        sb = C.sb
        C.ident = sb("ident", [128, 128])
        C.identb = sb("identb", [128, 128], BF16)
        C.memset("pool", C.ident[:], 1.0, ["ident"])
        S.op("pool", lambda e: e.affine_select(out=C.ident[:], in_=C.ident[:], pattern=[[-1, 128]], compare_op=ALU.is_equal, fill=0.0, base=0, channel_multiplier=1), reads=["ident"], writes=["ident"])
        C.cp("dve", C.identb[:], C.ident[:], ["ident"], ["identb"])
        Btr = C.bank("Btr", [128, 8, 128], BF16)
        Bq = C.bank("Bq", [128, 512])
        Bo = [C.bank("Bo0", [128, 512]), C.bank("Bo1", [128, 512])]
        Bu = [C.bank("Bu0", [128, 512]), C.bank("Bu1", [128, 512])]
        wub = sb("wub", [128, 8, 5632], BF16)
        wdb = sb("wdb", [128, 22, 1024], BF16)
        wob = sb("wob", [128, 8, 1024], BF16)
        wgb = sb("wgb", [128, 2, 256], BF16)
        stg = [sb("stg0", [128, 704]), sb("stg1", [128, 704])]
        cnt = [0]

        def ldcast(dst, src, n):
            i = cnt[0] % 2
            cnt[0] += 1
            C.load("ldw%d" % i, stg[i][:, 0:n], src, ["stg%d" % i])
            C.cp(("pool", "act")[i], dst, stg[i][:, 0:n], ["stg%d" % i], ["wts"])
        for kc in range(8):
            for q in range(2):
                ldcast(wob[:, kc, q * 512:(q + 1) * 512], wout[kc * 128:(kc + 1) * 128, q * 512:(q + 1) * 512], 512)
        for kc in range(2):
            ldcast(wgb[:, kc, :], wglu[kc * 128:(kc + 1) * 128, :], 256)
        for kc in range(8):
            for q in range(8):
                ldcast(wub[:, kc, q * 704:(q + 1) * 704], wup[kc * 128:(kc + 1) * 128, q * 704:(q + 1) * 704], 704)
        for j in range(22):
            for q in range(2):
                ldcast(wdb[:, j, q * 512:(q + 1) * 512], wdown[j * 128:(j + 1) * 128, q * 512:(q + 1) * 512], 512)
        n2b = sb("n2b", [128, 1024])
        C.load("ldc", n2b[:], n2w.partition_broadcast(128), ["n2b"])
        if final:
            fnb = sb("fnb", [128, 1024])
            C.load("ldc", fnb[:], fnw.partition_broadcast(128), ["fnb"])
        cw = sb("fcw_sb", [128, 44, 3])
        C.load("ldc", cw[:].rearrange("p a b -> p (a b)"), fcw[:, :], ["fcw"])
        cb = sb("fcb_sb", [128, 44])
        C.load("ldc", cb[:], fcb[:, :], ["fcb"])
        halo = sb("halo", [128, 22, 2, 2])
        C.memset("pool", halo[:], 0.0, ["halo"])
        xn = sb("xn", [128, 2, 1024])
        yf = sb("yf", [128, 4, TB])
        yb = sb("yb", [128, 8, TB], BF16)
        sig = sb("sig", [128, 2, TB])
        h2 = sb("h2", [128, 1024], BF16)
        h2T = sb("h2T", [128, 8, TB], BF16)
        junk = sb("junk", [128, 1024], BF16)
        ssq = sb("ssq", [128, 4])
        pre = sb("pre", [128, 2, 2 + TB])
        acc = sb("acc", [128, 2, TB])
        sg = sb("sg", [128, TB])
        actb = sb("actb", [128, 22, TB], BF16)
        yTv = yT.rearrange("(k p) t -> p k t", p=128)

        def rmsnorm(src, wname, wtile, dst, dname):
            C.act(junk[:], src, AF.Square, ["xn"], ["junk", "ssq"], accum=ssq[:, 0:1])
            C.act(ssq[:, 1:2], ssq[:, 0:1], AF.Sqrt, ["ssq"], ["ssq"], bias=EPS, scale=1.0 / 1024)
            C.recip(ssq[:, 1:2], ssq[:, 1:2], ["ssq"], ["ssq"])
            C.stt("dve", dst, src, ssq[:, 1:2], wtile[:], ALU.mult, ALU.mult, ["xn", "ssq", wname], [dname])

        for it in range(NT):
            tok0 = it * TB
            for tb in range(2):
                C.load("ldx%d" % tb, xn[:, tb, :], x[tok0 + tb * 128: tok0 + (tb + 1) * 128, :], ["xn"])
            for q in range(2):
                C.load("ldy", yf[:], yTv[:, 4 * q:4 * q + 4, tok0:tok0 + TB], ["yf"])
                C.cp("pool", yb[:, 4 * q:4 * q + 4, :], yf[:], ["yf"], ["yb"])
            for ob in range(2):
                for k2 in range(2):
                    C.mm(Bq[:, ob * TB:(ob + 1) * TB], wgb[:, k2, ob * 128:(ob + 1) * 128], yb[:, 3 + 4 * k2, :], ["wts", "yb"], ["Bq"], start=(k2 == 0), stop=(k2 == 1))
            C.act(sig[:].rearrange("p a b -> p (a b)"), Bq[:, 0:2 * TB], AF.Sigmoid, [], ["Bq", "sig"])
            for ob in range(2):
                C.tt("dve", yb[:, 3 + 4 * ob, :], yb[:, 3 + 4 * ob, :], sig[:, ob, :], ALU.mult, ["yb", "sig"], ["yb"])
            for tb in range(2):
                for nb in range(2):
                    bo = Bo[nb]
                    bn = "Bo%d" % nb
                    for kc in range(8):
                        C.mm(bo[:], yb[:, kc, tb * 128:(tb + 1) * 128], wob[:, kc, nb * 512:(nb + 1) * 512], ["wts", "yb"], [bn], start=(kc == 0), stop=(kc == 7))
                    C.tt("dve", xn[:, tb, nb * 512:(nb + 1) * 512], xn[:, tb, nb * 512:(nb + 1) * 512], bo[:], ALU.add, [], [bn, "xn"])
            for tb in range(2):
                rmsnorm(xn[:, tb, :], "n2b", n2b, h2[:], "h2")
                for kc in range(8):
                    C.tr(Btr[:, kc, :], h2[:, kc * 128:(kc + 1) * 128], C.identb[:], ["h2", "identb"], ["Btr"])
                C.cp("act", h2T[:, :, tb * 128:(tb + 1) * 128], Btr[:], [], ["Btr", "h2T"])
            for j in range(22):
                C.cp("pool", pre[:, :, 0:2], halo[:, j, :, :], ["halo"], ["pre"])
                for hf in range(2):
                    fb = j + 22 * hf
                    bu = Bu[hf]
                    bn = "Bu%d" % hf
                    for kc in range(8):
                        C.mm(bu[:, 0:TB], wub[:, kc, fb * 128:(fb + 1) * 128], h2T[:, kc, :], ["wts", "h2T"], [bn], start=(kc == 0), stop=(kc == 7))
                    C.cp("act", pre[:, hf, 2:2 + TB], bu[:, 0:TB], [], [bn, "pre"])
                C.cp("pool", halo[:, j, :, :], pre[:, :, TB:TB + 2], ["pre"], ["halo"])
                for hf in range(2):
                    fb = j + 22 * hf
                    C.ts("dve", acc[:, hf, :], pre[:, hf, 2:2 + TB], cw[:, fb, 2:3], cb[:, fb:fb + 1], ALU.mult, ALU.add, ["pre", "fcw", "fcb"], ["acc"])
                    C.stt("dve", acc[:, hf, :], pre[:, hf, 1:1 + TB], cw[:, fb, 1:2], acc[:, hf, :], ALU.mult, ALU.add, ["pre", "fcw", "acc"], ["acc"])
                    C.stt("dve", acc[:, hf, :], pre[:, hf, 0:TB], cw[:, fb, 0:1], acc[:, hf, :], ALU.mult, ALU.add, ["pre", "fcw", "acc"], ["acc"])
                C.act(sg[:], acc[:, 0, :], AF.Silu, ["acc"], ["sg"])
                C.tt("pool", actb[:, j, :], sg[:], acc[:, 1, :], ALU.mult, ["sg", "acc"], ["actb"])
            for tb in range(2):
                for nb in range(2):
                    bo = Bo[nb]
                    bn = "Bo%d" % nb
                    for j in range(22):
                        C.mm(bo[:], actb[:, j, tb * 128:(tb + 1) * 128], wdb[:, j, nb * 512:(nb + 1) * 512], ["wts", "actb"], [bn], start=(j == 0), stop=(j == 21))
                    C.tt("dve", xn[:, tb, nb * 512:(nb + 1) * 512], xn[:, tb, nb * 512:(nb + 1) * 512], bo[:], ALU.add, [], [bn, "xn"])
                if it >= 1:
                    if final:
                        rmsnorm(xn[:, tb, :], "fnb", fnb, xn[:, tb, :], "xn")
                    C.store("stx%d" % tb, xo[tok0 - TB + tb * 128: tok0 - TB + (tb + 1) * 128, :], xn[:, tb, :], ["xn"])
        S.finish(es)
    return nc


def prep_B(inp, l, xin, yTin, final):
    f = np.float32
    rows = []
    for hf in range(2):
        rows += [np.arange((2 * hf) * 128, (2 * hf + 1) * 128), np.arange((2 * hf + 1) * 128, (2 * hf + 2) * 128),
                 np.arange(512 + 128 * hf, 512 + 128 * hf + 128), np.arange(768 + 128 * hf, 768 + 128 * hf + 128)]
    wout = np.ascontiguousarray(inp["w_out"][l][np.concatenate(rows), :])
    fcw = np.ascontiguousarray(inp["ffn_conv_w"][l].reshape(3, 44, 128).transpose(2, 1, 0)).reshape(128, 132).astype(f)
    fcb = np.ascontiguousarray(inp["ffn_conv_b"][l].reshape(44, 128).T).astype(f)
    return {"x": np.ascontiguousarray(xin, f), "yT": np.ascontiguousarray(yTin, f), "wout": wout,
            "wglu": np.ascontiguousarray(inp["s5_w_glu"][l]), "n2w": inp["norm2_w"][l][None, :].astype(f),
            "fnw": inp["final_norm_w"][None, :].astype(f), "wup": np.ascontiguousarray(inp["w_up"][l]),
            "fcw": fcw, "fcb": fcb, "wdown": np.ascontiguousarray(inp["w_down"][l])}


def kernel(**inputs):
    inp = {k: np.asarray(v) for k, v in inputs.items()}
    x = np.asarray(inp["x"], np.float32)
    nb, L, D = x.shape
    half = L // 2
    ntile_b = half // TB + 1
    cores = list(range(8))
    for l in range(2):
        final = (l == 1)
        ncA = build_A(L)
        mapsA = [prep_A(inp, l, c % 2, x[c // 2]) for c in cores]
        resA = run_bass_kernel_spmd(ncA, mapsA, core_ids=cores).results
        mapsB = []
        for c in cores:
            b, hh = c // 2, c % 2
            yfull = np.concatenate([resA[2 * b]["yT"], resA[2 * b + 1]["yT"]], axis=0)
            xin = np.zeros((half + TB, D), np.float32)
            yin = np.zeros((1024, half + TB), np.float32)
            if hh == 0:
                xin[TB:] = x[b, 0:half]
                yin[:, TB:] = yfull[:, 0:half]
            else:
                xin[:] = x[b, half - TB:L]
                yin[:] = yfull[:, half - TB:L]
            mapsB.append(prep_B(inp, l, xin, yin, final))
        ncB = build_B(ntile_b, final)
        resB = run_bass_kernel_spmd(ncB, mapsB, core_ids=cores).results
        x = np.stack([np.concatenate([resB[2 * b]["xo"], resB[2 * b + 1]["xo"]], axis=0) for b in range(nb)])
    return x.astype(np.float32)
```
